# Optimizing a Trainium2 kernel written in Bass

```python
import math
import jax, jax.numpy as jnp
from jax import lax
import numpy as np

D_MODEL = 1024
BATCH = 32
SEQ = 2048
DEPTH = 2

EPS = 1e-6
ROPE_THETA = 10000.0
D_FF = 2816
CONV_K = 4
BLOCK = 128

D_MIX = 2 * D_MODEL
SSD_WIDTH = D_MIX // 2
SSD_HEAD_DIM = 64
SSD_HEADS = SSD_WIDTH // SSD_HEAD_DIM
SSD_GROUPS = 2
SSD_STATE = 128
SSD_CONV_DIM = SSD_WIDTH + 2 * SSD_GROUPS * SSD_STATE
SSD_CHUNK = 128
ML_WIDTH = D_MIX - SSD_WIDTH
ML_HEADS = 4
ML_V_DIM = ML_WIDTH // ML_HEADS
ML_QK_DIM = ML_V_DIM // 2
ML_CHUNK = 128
HY_SPLITS = (SSD_WIDTH, SSD_CONV_DIM, SSD_HEADS, ML_HEADS * ML_QK_DIM, ML_HEADS * ML_QK_DIM,
             ML_WIDTH, ML_WIDTH, ML_HEADS, ML_HEADS)
HY_IN = sum(HY_SPLITS)

ATT_HEAD_DIM = 64
ATT_HEADS = D_MODEL // ATT_HEAD_DIM
IDX_HEADS = ATT_HEADS // 2
IDX_DIM = ATT_HEAD_DIM
TOPK_MAX = 256
SA_SPLITS = (ATT_HEADS * ATT_HEAD_DIM, ATT_HEAD_DIM, ATT_HEAD_DIM, IDX_HEADS * IDX_DIM, IDX_DIM, IDX_HEADS)
SA_IN = sum(SA_SPLITS)

N_EVEN = (DEPTH + 1) // 2
N_ODD = DEPTH // 2

kernel_name = "hybrid_ssd_mlstm_dsa_macaron"


def split_cols(y, sizes):
    bounds, acc = [], 0
    for s in sizes[:-1]:
        acc += s
        bounds.append(acc)
    return jnp.split(y, bounds, axis=-1)


def rms_norm(x, g):
    xf = x.astype(jnp.float32)
    y = xf * lax.rsqrt(jnp.mean(xf * xf, axis=-1, keepdims=True) + EPS) * g.astype(jnp.float32)
    return y.astype(x.dtype)


def swiglu(h, w_gate, w_up, w_down):
    return (jax.nn.silu(h @ w_gate) * (h @ w_up)) @ w_down


def rope_tables(positions, dim):
    inv = ROPE_THETA ** (-jnp.arange(0, dim, 2, dtype=jnp.float32) / dim)
    ang = positions.astype(jnp.float32)[..., None] * inv
    return jnp.cos(ang), jnp.sin(ang)


def apply_rope(x, cos, sin):
    half = x.shape[-1] // 2
    xf = x.astype(jnp.float32)
    x1, x2 = xf[..., :half], xf[..., half:]
    c, s = cos[:, :, None, :], sin[:, :, None, :]
    return jnp.concatenate([x1 * c - x2 * s, x1 * s + x2 * c], axis=-1).astype(x.dtype)


def causal_dwconv(u, w, bias):
    L, K = u.shape[1], w.shape[0]
    up = jnp.pad(u, ((0, 0), (K - 1, 0), (0, 0)))
    return sum(up[:, k:k + L] * w[k] for k in range(K)) + bias


def ssd_chunked(x, dt, A, B, C, chunk):
    b, L, H, P = x.shape
    G, N = B.shape[-2:]
    E, nc = H // G, L // chunk
    a = (dt * A).reshape(b, nc, chunk, G, E)
    xd = (x * dt[..., None]).reshape(b, nc, chunk, G, E, P)
    B = B.reshape(b, nc, chunk, G, N)
    C = C.reshape(b, nc, chunk, G, N)
    acs = jnp.cumsum(a, axis=2)
    seg = acs[:, :, :, None] - acs[:, :, None, :]
    mask = jnp.tril(jnp.ones((chunk, chunk), bool))[:, :, None, None]
    Lm = jnp.exp(jnp.where(mask, seg, -jnp.inf))
    CB = jnp.einsum('bctgn,bcsgn->bctsg', C, B)
    y_diag = jnp.einsum('bctsg,bctsge,bcsgep->bctgep', CB, Lm, xd)
    decay_end = jnp.exp(acs[:, :, -1:] - acs)
    s_loc = jnp.einsum('bcsgn,bcsge,bcsgep->bcgepn', B, decay_end, xd)
    chunk_decay = jnp.exp(acs[:, :, -1])

    def step(hc, inp):
        dec, s = inp
        return hc * dec[..., None, None] + s, hc

    h0 = jnp.zeros((b, G, E, P, N), jnp.float32)
    _, h_in = lax.scan(step, h0, (jnp.moveaxis(chunk_decay, 1, 0), jnp.moveaxis(s_loc, 1, 0)))
    h_in = jnp.moveaxis(h_in, 0, 1)
    y_off = jnp.einsum('bctgn,bcgepn,bctge->bctgep', C, h_in, jnp.exp(acs))
    return (y_diag + y_off).reshape(b, L, H, P)


def mlstm_chunked(q, k, v, i_pre, f_pre, chunk):
    b, L, H, dk = q.shape
    dv = v.shape[-1]
    nc = L // chunk
    q = (q.astype(jnp.float32) * dk ** -0.5).reshape(b, nc, chunk, H, dk)
    k = k.astype(jnp.float32).reshape(b, nc, chunk, H, dk)
    v = v.astype(jnp.float32).reshape(b, nc, chunk, H, dv)
    li = i_pre.astype(jnp.float32).reshape(b, nc, chunk, H)
    lf = jax.nn.log_sigmoid(f_pre.astype(jnp.float32)).reshape(b, nc, chunk, H)
    bcs = jnp.cumsum(lf, axis=2)
    b_last = bcs[:, :, -1]
    g = b_last[:, :, None] - bcs + li
    m_loc = jnp.max(g, axis=2)
    wg = jnp.exp(g - m_loc[:, :, None])
    c_loc = jnp.einsum('bcsh,bcshv,bcshk->bchvk', wg, v, k)
    n_loc = jnp.einsum('bcsh,bcshk->bchk', wg, k)

    def step(carry, inp):
        Cs, ns, ms = carry
        bl, cl, nl, ml = inp
        m_new = jnp.maximum(bl + ms, ml)
        a = jnp.exp(bl + ms - m_new)
        gg = jnp.exp(ml - m_new)
        new = (a[..., None, None] * Cs + gg[..., None, None] * cl,
               a[..., None] * ns + gg[..., None] * nl, m_new)
        return new, (Cs, ns, ms)

    init = (jnp.zeros((b, H, dv, dk), jnp.float32), jnp.zeros((b, H, dk), jnp.float32),
            jnp.zeros((b, H), jnp.float32))
    xs = tuple(jnp.moveaxis(t, 1, 0) for t in (b_last, c_loc, n_loc, m_loc))
    _, (c_in, n_in, m_in) = lax.scan(step, init, xs)
    c_in, n_in, m_in = (jnp.moveaxis(t, 0, 1) for t in (c_in, n_in, m_in))
    Dm = bcs[:, :, :, None, :] - bcs[:, :, None, :, :] + li[:, :, None, :, :]
    mask = jnp.tril(jnp.ones((chunk, chunk), bool))[:, :, None]
    Dm = jnp.where(mask, Dm, -jnp.inf)
    inter = bcs + m_in[:, :, None, :]
    m_t = jnp.maximum(jnp.max(Dm, axis=3), inter)
    S = jnp.einsum('bcthk,bcshk->bctsh', q, k) * jnp.exp(Dm - m_t[:, :, :, None, :])
    w_inter = jnp.exp(inter - m_t)
    num = jnp.einsum('bctsh,bcshv->bcthv', S, v) + w_inter[..., None] * jnp.einsum('bcthk,bchvk->bcthv', q, c_in)
    den = jnp.sum(S, axis=3) + w_inter * jnp.einsum('bcthk,bchk->bcth', q, n_in)
    h = num / jnp.maximum(jnp.abs(den), jnp.exp(-m_t))[..., None]
    return h.reshape(b, L, H, dv)


def hybrid_mixer(h, w_in, conv_w, conv_b, dt_bias, a_log, d_skip, ssd_norm,
                 igate_bias, fgate_bias, mlstm_norm, w_out):
    b, L, _ = h.shape
    z, xbc, dt_raw, qm, km, vm, om, ig, fg = split_cols(h @ w_in, HY_SPLITS)
    xbc = jax.nn.silu(causal_dwconv(xbc, conv_w, conv_b))
    xs, bm, cm = split_cols(xbc, (SSD_WIDTH, SSD_GROUPS * SSD_STATE, SSD_GROUPS * SSD_STATE))
    xs = xs.reshape(b, L, SSD_HEADS, SSD_HEAD_DIM).astype(jnp.float32)
    dt = jax.nn.softplus(dt_raw.astype(jnp.float32) + dt_bias.astype(jnp.float32))
    A = -jnp.exp(a_log.astype(jnp.float32))
    y = ssd_chunked(xs, dt, A,
                    bm.reshape(b, L, SSD_GROUPS, SSD_STATE).astype(jnp.float32),
                    cm.reshape(b, L, SSD_GROUPS, SSD_STATE).astype(jnp.float32), SSD_CHUNK)
    y = y + xs * d_skip.astype(jnp.float32)[:, None]
    gate = jax.nn.silu(z.astype(jnp.float32)).reshape(b, L, SSD_GROUPS, -1)
    y = rms_norm(y.reshape(b, L, SSD_GROUPS, -1) * gate, ssd_norm.reshape(SSD_GROUPS, -1))
    y = y.reshape(b, L, SSD_WIDTH)
    hm = mlstm_chunked(qm.reshape(b, L, ML_HEADS, ML_QK_DIM), km.reshape(b, L, ML_HEADS, ML_QK_DIM),
                       vm.reshape(b, L, ML_HEADS, ML_V_DIM), ig + igate_bias, fg + fgate_bias, ML_CHUNK)
    hm = rms_norm(hm, mlstm_norm.reshape(ML_HEADS, ML_V_DIM)).reshape(b, L, ML_WIDTH)
    hm = hm * jax.nn.sigmoid(om.astype(jnp.float32))
    return jnp.concatenate([y, hm], axis=-1).astype(h.dtype) @ w_out


def sparse_attention(h, cos, sin, w_in, q_norm, k_norm, w_out, topk):
    b, L, _ = h.shape
    q, k, v, qi, ki, wi = split_cols(h @ w_in, SA_SPLITS)
    q = apply_rope(rms_norm(q.reshape(b, L, ATT_HEADS, ATT_HEAD_DIM), q_norm), cos, sin)
    k = apply_rope(rms_norm(k, k_norm)[:, :, None, :], cos, sin)[:, :, 0]
    qi = apply_rope(qi.reshape(b, L, IDX_HEADS, IDX_DIM), cos, sin)
    ki = apply_rope(ki[:, :, None, :], cos, sin)[:, :, 0]
    kv = jnp.concatenate([k, v], axis=-1)
    nb = L // BLOCK
    key_pos = jnp.arange(L)
    idx_scale = (IDX_HEADS ** -0.5) * (IDX_DIM ** -0.5)
    att_scale = ATT_HEAD_DIM ** -0.5

    def to_blocks(t):
        return jnp.moveaxis(t.reshape(b, nb, BLOCK, *t.shape[2:]), 1, 0)

    def block(args):
        j, qb, qib, wib = args
        q_pos = j * BLOCK + jnp.arange(BLOCK)
        logits = jnp.einsum('bthd,bsd->btsh', qib, ki).astype(jnp.float32)
        score = jnp.einsum('btsh,bth->bts', jax.nn.relu(logits), wib.astype(jnp.float32)) * idx_scale
        causal = key_pos[None, :] <= q_pos[:, None]
        score = jnp.where(causal[None], score, -jnp.inf)
        _, sel = lax.top_k(score, topk)
        valid = sel <= q_pos[None, :, None]
        kv_sel = jax.vmap(lambda kvb, sb: kvb[sb])(kv, sel)
        k_sel, v_sel = kv_sel[..., :ATT_HEAD_DIM], kv_sel[..., ATT_HEAD_DIM:]
        s = jnp.einsum('bthd,btkd->bthk', qb, k_sel).astype(jnp.float32) * att_scale
        s = jnp.where(valid[:, :, None, :], s, -jnp.inf)
        p = jax.nn.softmax(s, axis=-1).astype(v_sel.dtype)
        return jnp.einsum('bthk,btkd->bthd', p, v_sel)

    o = lax.map(block, (jnp.arange(nb), to_blocks(q), to_blocks(qi), to_blocks(wi)))
    o = jnp.moveaxis(o, 0, 1).reshape(b, L, ATT_HEADS * ATT_HEAD_DIM)
    return o @ w_out


def setup_inputs(seed: int = 0) -> dict:
    key = jax.random.key(seed)
    ks = jax.random.split(key, 24)
    f32 = jnp.float32

    def nrm(k, shape, scale):
        return jax.random.normal(k, shape, f32) * scale

    def gain(k, shape):
        return 1.0 + 0.02 * jax.random.normal(k, shape, f32)

    x = nrm(ks[0], (BATCH, SEQ, D_MODEL), 1.0)
    positions = (jax.random.randint(ks[1], (BATCH, 1), 0, 4096, jnp.int32)
                 + jnp.arange(SEQ, dtype=jnp.int32)[None, :])
    dt0 = jnp.exp(jax.random.uniform(ks[10], (N_EVEN, SSD_HEADS), f32, math.log(1e-3), math.log(1e-1)))
    return {
        "x": x,
        "positions": positions,
        "ffn_norm": gain(ks[2], (DEPTH, 2, D_MODEL)),
        "ffn_w_gate": nrm(ks[3], (DEPTH, 2, D_MODEL, D_FF), D_MODEL ** -0.5),
        "ffn_w_up": nrm(ks[4], (DEPTH, 2, D_MODEL, D_FF), D_MODEL ** -0.5),
        "ffn_w_down": nrm(ks[5], (DEPTH, 2, D_FF, D_MODEL), D_FF ** -0.5),
        "mix_norm": gain(ks[6], (DEPTH, D_MODEL)),
        "hy_w_in": nrm(ks[7], (N_EVEN, D_MODEL, HY_IN), D_MODEL ** -0.5),
        "hy_conv_w": nrm(ks[8], (N_EVEN, CONV_K, SSD_CONV_DIM), CONV_K ** -0.5),
        "hy_conv_b": nrm(ks[9], (N_EVEN, SSD_CONV_DIM), 0.01),
        "hy_dt_bias": jnp.log(jnp.expm1(dt0)),
        "hy_a_log": jnp.log(jax.random.uniform(ks[11], (N_EVEN, SSD_HEADS), f32, 1.0, 16.0)),
        "hy_d_skip": gain(ks[12], (N_EVEN, SSD_HEADS)),
        "hy_ssd_norm": gain(ks[13], (N_EVEN, SSD_WIDTH)),
        "hy_igate_bias": nrm(ks[14], (N_EVEN, ML_HEADS), 0.1),
        "hy_fgate_bias": jnp.linspace(3.0, 6.0, ML_HEADS, dtype=f32)[None, :] + nrm(ks[15], (N_EVEN, ML_HEADS), 0.1),
        "hy_mlstm_norm": gain(ks[16], (N_EVEN, ML_WIDTH)),
        "hy_w_out": nrm(ks[17], (N_EVEN, D_MIX, D_MODEL), D_MIX ** -0.5),
        "sa_w_in": nrm(ks[18], (N_ODD, D_MODEL, SA_IN), D_MODEL ** -0.5),
        "sa_q_norm": gain(ks[19], (N_ODD, ATT_HEAD_DIM)),
        "sa_k_norm": gain(ks[20], (N_ODD, ATT_HEAD_DIM)),
        "sa_w_out": nrm(ks[21], (N_ODD, ATT_HEADS * ATT_HEAD_DIM, D_MODEL), (ATT_HEADS * ATT_HEAD_DIM) ** -0.5),
    }


def reference(x, positions, ffn_norm, ffn_w_gate, ffn_w_up, ffn_w_down, mix_norm,
              hy_w_in, hy_conv_w, hy_conv_b, hy_dt_bias, hy_a_log, hy_d_skip, hy_ssd_norm,
              hy_igate_bias, hy_fgate_bias, hy_mlstm_norm, hy_w_out,
              sa_w_in, sa_q_norm, sa_k_norm, sa_w_out):
    cos, sin = rope_tables(positions, ATT_HEAD_DIM)
    topk = min(TOPK_MAX, x.shape[1] // 4)
    for l in range(DEPTH):
        x = x + 0.5 * swiglu(rms_norm(x, ffn_norm[l, 0]), ffn_w_gate[l, 0], ffn_w_up[l, 0], ffn_w_down[l, 0])
        h = rms_norm(x, mix_norm[l])
        if l % 2 == 0:
            e = l // 2
            m = hybrid_mixer(h, hy_w_in[e], hy_conv_w[e], hy_conv_b[e], hy_dt_bias[e], hy_a_log[e],
                             hy_d_skip[e], hy_ssd_norm[e], hy_igate_bias[e], hy_fgate_bias[e],
                             hy_mlstm_norm[e], hy_w_out[e])
        else:
            o = l // 2
            m = sparse_attention(h, cos, sin, sa_w_in[o], sa_q_norm[o], sa_k_norm[o], sa_w_out[o], topk)
        x = x + m
        x = x + 0.5 * swiglu(rms_norm(x, ffn_norm[l, 1]), ffn_w_gate[l, 1], ffn_w_up[l, 1], ffn_w_down[l, 1])
    return x
```

```python
import numpy as np
from contextlib import ExitStack
import concourse.bass as bass
import concourse.mybir as mybir
from concourse.bass_utils import run_bass_kernel_spmd

F32 = mybir.dt.float32
BF16 = mybir.dt.bfloat16
I32 = mybir.dt.int32
AF = mybir.ActivationFunctionType
ALU = mybir.AluOpType
AX = mybir.AxisListType

PE, ACT, DVE, POOL, SP = "tensor", "scalar", "vector", "gpsimd", "sync"
ENGS = (PE, ACT, DVE, POOL, SP)

D = 1024
L = 2048
DFF = 2816
NFC = DFF // 128
NCORES = 8
EPS = 1e-6


class Dep:
    __slots__ = ("w", "r", "name", "excl")

    def __init__(self, name="", excl=False):
        self.w = None
        self.r = {}
        self.name = name
        self.excl = excl


class Sched:
    SEM_CAP = 30000

    def __init__(self, nc, stack):
        self.nc = nc
        self.stack = stack
        self.streams = {e: [] for e in ENGS}
        self.semh = {}
        self.gen = {e: 0 for e in ENGS}
        self.cnt = {e: 0 for e in ENGS}
        self.seen = {e: {} for e in ENGS}
        self.dcnt = {}
        self.dgen = {}
        self.nsem = 0
        for e in ENGS:
            self._newsem((e, 0))

    def _newsem(self, key):
        h = self.stack.enter_context(self.nc.semaphore("s%d" % self.nsem))
        self.nsem += 1
        self.semh[key] = h
        return h

    def op(self, eng, fn, reads=(), writes=(), dma=None):
        if any(d.excl for d in reads):
            writes = tuple(writes) + tuple(d for d in reads if d.excl)
            reads = tuple(d for d in reads if not d.excl)
        waits = {}
        seen = self.seen[eng]
        is_dma = dma is not None

        def need(sk, val, src, kind):
            if (not is_dma) and src == eng:
                if eng == PE or kind != "raw":
                    return
            if sk[0] == "dma":
                val = self.dcnt[sk]
            if seen.get(sk, 0) >= val:
                return
            if waits.get(sk, 0) < val:
                waits[sk] = val

        for d in reads:
            if d.w is not None:
                need(d.w[0], d.w[1], d.w[2], "raw")
        for d in writes:
            if d.w is not None:
                need(d.w[0], d.w[1], d.w[2], "waw")
            for sk, (val, src) in d.r.items():
                need(sk, val, src, "war")
        for sk, val in waits.items():
            seen[sk] = val
        if is_dma:
            g = self.dgen.get(dma, 0)
            sk = ("dma", dma, g)
            if sk in self.semh and self.dcnt[sk] + 16 > self.SEM_CAP:
                g += 1
                self.dgen[dma] = g
                sk = ("dma", dma, g)
            if sk not in self.semh:
                self._newsem(sk)
                self.dcnt[sk] = 0
            self.dcnt[sk] += 16
            ev = (sk, self.dcnt[sk], None)
            inc = 16
        else:
            if self.cnt[eng] >= self.SEM_CAP:
                self.gen[eng] += 1
                self.cnt[eng] = 0
                self._newsem((eng, self.gen[eng]))
            self.cnt[eng] += 1
            sk = (eng, self.gen[eng])
            ev = (sk, self.cnt[eng], eng)
            inc = 1
        self.streams[eng].append((tuple(waits.items()), fn, sk, inc))
        for d in reads:
            d.r[ev[0]] = (ev[1], ev[2])
        for d in writes:
            d.w = ev
            d.r = {}
        return ev

    def final_wait(self, eng, deps):
        waits = {}
        for d in deps:
            evs = list(d.r.items())
            if d.w is not None:
                evs.append((d.w[0], (d.w[1], d.w[2])))
            for sk, (val, src) in evs:
                if sk[0] == "dma":
                    val = self.dcnt[sk]
                if waits.get(sk, 0) < val:
                    waits[sk] = val
        self.streams[eng].append((tuple(waits.items()), None, None, 0))

    def emit(self):
        nc = self.nc
        with nc.Block() as block:
            for eng in ENGS:
                stream = self.streams[eng]
                if not stream:
                    continue

                def body(e, stream=stream):
                    semh = self.semh
                    for waits, fn, sk, inc in stream:
                        for wk, wv in waits:
                            e.wait_ge(semh[wk], wv)
                        if fn is not None:
                            ins = fn(e)
                            ins.then_inc(semh[sk], inc)

                getattr(block, eng)(body)


class K:
    def __init__(self, nc, stack):
        self.nc = nc
        self.stack = stack
        self.s = Sched(nc, stack)
        self.nalloc = 0

    def sb(self, shape, dt, name=None):
        self.nalloc += 1
        return self.stack.enter_context(
            self.nc.sbuf_tensor(name or ("t%d" % self.nalloc), list(shape), dt))

    def ps(self, shape, dt, name=None):
        self.nalloc += 1
        return self.stack.enter_context(
            self.nc.psum_tensor(name or ("p%d" % self.nalloc), list(shape), dt))


def _dsize(dt):
    return 4 if dt in (F32, I32) else 2


class Arena:
    def __init__(self, k, nbytes):
        self.t = k.sb([128, nbytes // 4], F32, "arena")
        self.n4 = nbytes // 4
        self.off = 0
        self.deps = []
        self.inherit = {}

    def mark(self):
        return self.off

    def release(self, m=0):
        keep = []
        for off, d in self.deps:
            if off >= m:
                evs = list(d.r.items())
                if d.w is not None:
                    evs.append((d.w[0], (d.w[1], d.w[2])))
                for sk, (v, src) in evs:
                    if self.inherit.get(sk, (0, None))[0] < v:
                        self.inherit[sk] = (v, src)
            else:
                keep.append((off, d))
        self.deps = keep
        self.off = m

    def alloc(self, shape, dt, ndeps=1):
        P = shape[0]
        nel = int(np.prod(shape[1:]))
        n4 = (nel * _dsize(dt) + 3) // 4
        n4 = (n4 + 7) // 8 * 8
        assert self.off + n4 <= self.n4, ("arena overflow", self.off, n4, self.n4)
        v = self.t[:, self.off:self.off + n4]
        if dt != F32:
            v = v.bitcast(dt)
        v = v[:, 0:nel]
        if len(shape) == 3:
            v = v.rearrange("p (a b) -> p a b", a=shape[1])
        elif len(shape) == 4:
            v = v.rearrange("p (a b c) -> p a b c", a=shape[1], b=shape[2])
        if P < 128:
            v = v[0:P]
        deps = []
        for _ in range(ndeps):
            d = Dep()
            d.r = dict(self.inherit)
            self.deps.append((self.off, d))
            deps.append(d)
        self.off += n4
        return (v, deps[0]) if ndeps == 1 else (v, deps)


class WStream:
    def __init__(self, k, nslots, stage_elems, name, nstage=2):
        self.k = k
        self.n = nslots
        self.ns = nstage
        self.elems = stage_elems
        self.stage = [k.sb([128, stage_elems], F32, "%s_st%d" % (name, i)) for i in range(nstage)]
        self.wb = [k.sb([128, stage_elems], BF16, "%s_wb%d" % (name, i)) for i in range(nslots)]
        self.dst = [Dep() for _ in range(nstage)]
        self.dwb = [Dep() for _ in range(nslots)]
        self.i = 0
        self.name = name

    def load(self, srcs):
        k = self.k
        slot = self.i % self.n
        ss = self.i % self.ns
        self.i += 1
        st, wb = self.stage[ss], self.wb[slot]
        off = 0
        for ap, shape in srcs:
            n = int(np.prod(shape))
            dstv = st[:, off:off + n]
            if len(shape) == 2:
                dstv = dstv.rearrange("p (a b) -> p a b", a=shape[0])
            k.s.op(SP, lambda e, o=dstv, i=ap: e.dma_start(out=o, in_=i),
                   reads=(), writes=(self.dst[ss],), dma="%s%d" % (self.name, ss))
            off += n
        tot = off
        k.s.op(POOL, lambda e, o=wb[:, 0:tot], i=st[:, 0:tot]: e.tensor_copy(out=o, in_=i),
               reads=(self.dst[ss],), writes=(self.dwb[slot],))
        return wb, self.dwb[slot]


ARENA_BYTES = 62 * 1024
C_Z, C_XS, C_B, C_C, C_DT, C_Q, C_K, C_V, C_O, C_I, C_F = 0, 1024, 2048, 2304, 2560, 2576, 3088, 3600, 4624, 5648, 5652
S_Q, S_K, S_V, S_QI, S_KI, S_WI = 0, 1024, 1088, 1152, 1664, 1728


class Prog:
    def __init__(self, nseq, stages, names):
        self.nseq = nseq
        self.stages = stages
        nc = bass.Bass("TRN2", target_bir_lowering=False)
        self.nc = nc
        self.inp = {}
        for nm, shape, dt in names:
            self.inp[nm] = nc.dram_tensor(nm, list(shape), dt, kind="ExternalInput").ap()
        self.out = nc.dram_tensor("yT", [nseq, D, L], F32, kind="ExternalOutput").ap()
        self.scr = nc.dram_tensor("scr_rows", [20, L], F32, kind="Internal").ap()
        self.scrdep = Dep("scr")

    def build(self):
        nc = self.nc
        with ExitStack() as stack:
            k = K(nc, stack)
            self.k = k
            self.alloc()
            for q in range(self.nseq):
                self.load_x(q)
                for st in self.stages:
                    if st[0] == "ffn":
                        self.ffn(st[1], st[2])
                    elif st[0] == "hy":
                        self.hybrid()
                    elif st[0] == "sa":
                        self.sattn(q)
                self.store_x(q)
            k.s.final_wait(SP, self.xdep_flat())
            k.s.emit()
        return nc

    def op(self, eng, fn, r=(), w=(), dma=None):
        return self.k.s.op(eng, fn, reads=r, writes=w, dma=dma)

    def alloc(self):
        k = self.k
        inp = self.inp
        self.xT = k.sb([128, 8, L], F32, "xTs")
        self.xdep = [[Dep("x%d_%d" % (c, t)) for t in range(4)] for c in range(8)]
        self.hT = k.sb([128, 8, L], BF16, "hTs")
        self.hdep = [Dep("h%d" % t) for t in range(4)]
        self.ws = WStream(k, 3, 2048, "w")
        self.ar = Arena(k, ARENA_BYTES)
        self.psum = [k.ps([128, 512], F32, "ps%d" % i) for i in range(6)]
        self.pdep = [Dep("ps%d" % i, excl=True) for i in range(6)]
        self.pi = 0
        self.pT16 = [k.ps([128, 1024], BF16, "pT16_%d" % i) for i in range(2)]
        self.pT16dep = [Dep("pT%d" % i, excl=True) for i in range(2)]
        self.pT16i = 0
        self.cdep = Dep("const")
        self.ones = k.sb([128, 128], BF16, "ones")
        self.op(POOL, lambda e: e.memset(self.ones[:, :], 1.0), w=(self.cdep,))
        self.epsc = k.sb([128, 1], F32, "epsc")
        self.op(POOL, lambda e: e.memset(self.epsc[:, :], EPS), w=(self.cdep,))
        self.onesf = k.sb([128, 128], F32, "onesf")
        self.op(POOL, lambda e: e.memset(self.onesf[:, :], 1.0), w=(self.cdep,))
        self.identf = k.sb([128, 128], F32, "identf")
        self.op(POOL, lambda e: e.affine_select(out=self.identf[:, :], in_=self.onesf[:, :], pattern=[[1, 128]],
                                                compare_op=ALU.is_equal, fill=0.0, base=0, channel_multiplier=-1),
                r=(self.cdep,), w=(self.cdep,))
        self.identb = k.sb([128, 128], BF16, "identb")
        self.op(POOL, lambda e: e.tensor_copy(out=self.identb[:, :], in_=self.identf[:, :]),
                r=(self.cdep,), w=(self.cdep,))
        self.trif = k.sb([128, 128], F32, "trif")
        self.op(POOL, lambda e: e.affine_select(out=self.trif[:, :], in_=self.onesf[:, :], pattern=[[1, 128]],
                                                compare_op=ALU.is_ge, fill=0.0, base=0, channel_multiplier=-1),
                r=(self.cdep,), w=(self.cdep,))
        self.trib = k.sb([128, 128], BF16, "trib")
        self.op(POOL, lambda e: e.tensor_copy(out=self.trib[:, :], in_=self.trif[:, :]),
                r=(self.cdep,), w=(self.cdep,))
        self.gvec = k.sb([128, 8], F32, "gvec")
        self.gdep = Dep("g")
        self.sq = [k.sb([128, 512], BF16, "sq%d" % i) for i in range(2)]
        self.sqdep = [Dep() for _ in range(2)]
        self.sqi = 0
        self.rstd = k.sb([128, 512], F32, "rstd")
        self.rdep = Dep("rstd")
        self.sg = [k.sb([128, 512], F32, "sg%d" % i) for i in range(2)]
        self.sgdep = [Dep() for _ in range(2)]
        self.sgi = 0
        kinds = set(st[0] for st in self.stages)
        if "hy" in kinds:
            self.alloc_hy_params()
        if "sa" in kinds:
            self.alloc_sa_params()

    def small_load(self, name, shape, src, slow=False):
        k = self.k
        t = k.sb(shape, F32, name)
        if slow:
            self.op(SP, lambda e: e.dma_start(out=t[:], in_=src, allow_slow_non_contiguous=True),
                    w=(self.cdep,), dma="par")
        else:
            self.op(SP, lambda e: e.dma_start(out=t[:], in_=src), w=(self.cdep,), dma="par")
        return t

    def alloc_hy_params(self):
        inp = self.inp
        k = self.k
        self.convw = k.sb([128, 4, 12], F32, "convw")
        for kx in range(4):
            self.op(SP, lambda e, kx=kx: e.dma_start(out=self.convw[:, kx, :],
                                                     in_=inp["hy_conv_w"][0, kx].rearrange("(c p) -> p c", p=128),
                                                     allow_slow_non_contiguous=True), w=(self.cdep,), dma="par")
        self.convb = self.small_load("convb", [128, 12], inp["hy_conv_b"][0].rearrange("(c p) -> p c", p=128),
                                     slow=True)
        self.dtb = self.small_load("dtb", [16, 1], inp["hy_dt_bias"][0].rearrange("(h o) -> h o", o=1))
        self.alog = self.small_load("alog", [16, 1], inp["hy_a_log"][0].rearrange("(h o) -> h o", o=1))
        self.ahalf = k.sb([16, 1], F32, "ahalf")
        self.op(ACT, lambda e: e.activation(out=self.ahalf[:, :], in_=self.alog[:, :], func=AF.Exp),
                r=(self.cdep,), w=(self.cdep,))
        self.op(DVE, lambda e: e.tensor_scalar(out=self.ahalf[:, :], in0=self.ahalf[:, :], scalar1=-0.5, scalar2=None,
                                               op0=ALU.mult), r=(self.cdep,), w=(self.cdep,))
        dsk = inp["hy_d_skip"]
        self.dskip = k.sb([128, 8], F32, "dskip")
        for hh in range(2):
            src = bass.AP(dsk.tensor, hh, [[0, 64], [2, 8]])
            self.op(SP, lambda e, hh=hh, src=src: e.dma_start(out=self.dskip[hh * 64:(hh + 1) * 64, :], in_=src,
                                                               allow_slow_non_contiguous=True),
                    w=(self.cdep,), dma="par")
        self.ssdn = self.small_load("ssdn", [128, 8], inp["hy_ssd_norm"][0].rearrange("(c p) -> p c", p=128), slow=True)
        self.mln = self.small_load("mln", [128, 8], inp["hy_mlstm_norm"][0].rearrange("(c p) -> p c", p=128), slow=True)
        self.igb = self.small_load("igb", [4, 1], inp["hy_igate_bias"][0].rearrange("(h o) -> h o", o=1))
        self.fgb = self.small_load("fgb", [4, 1], inp["hy_fgate_bias"][0].rearrange("(h o) -> h o", o=1))
        self.nfgb = k.sb([4, 1], F32, "nfgb")
        self.op(DVE, lambda e: e.tensor_scalar(out=self.nfgb[:, :], in0=self.fgb[:, :], scalar1=-1.0, scalar2=None,
                                               op0=ALU.mult), r=(self.cdep,), w=(self.cdep,))
        self.carry16 = k.sb([16, 1], F32, "carry16")
        self.carry4 = k.sb([4, 1], F32, "carry4")
        self.c16dep = Dep()
        self.c4dep = Dep()

    def alloc_sa_params(self):
        import math
        k = self.k
        inp = self.inp
        op = self.op
        cd = self.cdep
        self.qng = k.sb([128, 1], F32, "qng")
        self.kng = k.sb([128, 1], F32, "kng")
        for hh in range(2):
            op(SP, lambda e, hh=hh: e.dma_start(out=self.qng[hh * 64:(hh + 1) * 64, :],
                                                in_=inp["sa_q_norm"][0].rearrange("(d o) -> d o", o=1)),
               w=(cd,), dma="par")
            op(SP, lambda e, hh=hh: e.dma_start(out=self.kng[hh * 64:(hh + 1) * 64, :],
                                                in_=inp["sa_k_norm"][0].rearrange("(d o) -> d o", o=1)),
               w=(cd,), dma="par")
        self.bd = k.sb([128, 128], BF16, "bd")
        op(POOL, lambda e: e.memset(self.bd[:, :], 0.0), w=(cd,))
        op(POOL, lambda e: e.memset(self.bd[0:64, 0:64], 1.0), w=(cd,))
        op(POOL, lambda e: e.memset(self.bd[64:128, 64:128], 1.0), w=(cd,))
        self.negf = k.sb([128, 32], F32, "negf")
        op(POOL, lambda e: e.memset(self.negf[:, :], -1.0), w=(cd,))
        self.rotf = k.sb([128, 128], F32, "rotf")
        op(POOL, lambda e: e.memset(self.rotf[:, :], 0.0), w=(cd,))
        for m0 in (32, 96):
            op(POOL, lambda e, m0=m0: e.affine_select(out=self.rotf[:, m0:m0 + 32], in_=self.onesf[:, 0:32],
                                                      pattern=[[-1, 32]], compare_op=ALU.is_equal, fill=0.0,
                                                      base=32 - m0, channel_multiplier=1), r=(cd,), w=(cd,))
        for m0 in (0, 64):
            op(POOL, lambda e, m0=m0: e.affine_select(out=self.rotf[:, m0:m0 + 32], in_=self.negf[:, :],
                                                      pattern=[[-1, 32]], compare_op=ALU.is_equal, fill=0.0,
                                                      base=-(m0 + 32), channel_multiplier=1), r=(cd,), w=(cd,))
        self.rotm = k.sb([128, 128], BF16, "rotm")
        op(POOL, lambda e: e.tensor_copy(out=self.rotm[:, :], in_=self.rotf[:, :]), r=(cd,), w=(cd,))
        self.invF = k.sb([128, 32], F32, "invF")
        self.invM = k.sb([128, 32], F32, "invM")
        self.invZ = k.sb([128, 32], F32, "invZ")
        self.invS = k.sb([128, 32], F32, "invS")
        for j in range(32):
            op(POOL, lambda e, j=j: e.memset(self.invF[:, j:j + 1], float(10000.0 ** (-j / 32.0))), w=(cd,))
        op(POOL, lambda e: e.memset(self.invZ[:, :], 0.0), w=(cd,))
        for blk in range(4):
            op(POOL, lambda e, blk=blk: e.affine_select(out=self.invM[blk * 32:(blk + 1) * 32, :],
                                                        in_=self.onesf[blk * 32:(blk + 1) * 32, 0:32],
                                                        pattern=[[1, 32]], compare_op=ALU.is_equal, fill=0.0,
                                                        base=0, channel_multiplier=-1), r=(cd,), w=(cd,))
        op(DVE, lambda e: e.tensor_tensor(out=self.invF[:, :], in0=self.invF[:, :], in1=self.invM[:, :], op=ALU.mult),
           r=(cd,), w=(cd,))
        op(DVE, lambda e: e.tensor_tensor_scan(out=self.invS[:, :], data0=self.invF[:, :], data1=self.invZ[:, :],
                                               initial=0.0, op0=ALU.add, op1=ALU.add), r=(cd,), w=(cd,))
        self.invf = self.invS[:, 31:32]
        self.mcol = [k.sb([128, 1], F32, "mcol%d" % i) for i in range(2)]
        for i in range(2):
            op(POOL, lambda e, i=i: e.memset(self.mcol[i][:, :], 0.0), w=(cd,))
            op(POOL, lambda e, i=i: e.memset(self.mcol[i][i * 64:(i + 1) * 64, :], 1.0), w=(cd,))
        self.NIT = 14
        self.pwr = k.sb([128, self.NIT + 1], F32, "pwr")
        for i in range(self.NIT + 1):
            op(POOL, lambda e, i=i: e.memset(self.pwr[:, i:i + 1], 2.0 ** -(i + 1)), w=(cd,))

    def negreg(self, e):
        if getattr(self, "_negreg", None) is None:
            self._negreg = e.to_reg(-1e30)
        return self._negreg

    def xdep_flat(self):
        return [d for row in self.xdep for d in row]

    def pbank(self, lo=0, hi=6):
        i = lo + self.pi % (hi - lo)
        self.pi += 1
        return self.psum[i], self.pdep[i]

    def pT(self):
        i = self.pT16i
        self.pT16i += 1
        b, blk = i % 2, (i // 2) % 8
        return self.pT16[b][:, blk * 128:(blk + 1) * 128], self.pT16dep[b]

    def load_x(self, q):
        src = self.inp["xT"]
        for c in range(8):
            self.op(SP, lambda e, c=c: e.dma_start(out=self.xT[:, c, :], in_=src[q, c * 128:(c + 1) * 128, :]),
                    w=tuple(self.xdep[c]), dma="x%d" % c)

    def store_x(self, q):
        for c in range(8):
            self.op(SP, lambda e, c=c: e.dma_start(out=self.out[q, c * 128:(c + 1) * 128, :], in_=self.xT[:, c, :]),
                    r=tuple(self.xdep[c]), dma="x%d" % c)

    def norm_to_hT(self, gsrc):
        op = self.op
        op(SP, lambda e: e.dma_start(out=self.gvec[:, :], in_=gsrc.rearrange("(c p) -> p c", p=128),
                                     allow_slow_non_contiguous=True), w=(self.gdep,), dma="g")
        for t in range(4):
            ts = slice(t * 512, (t + 1) * 512)
            pt, pd = self.pbank()
            for c in range(8):
                i = self.sqi % 2
                self.sqi += 1
                op(ACT, lambda e, i=i, c=c, ts=ts: e.activation(out=self.sq[i][:, :], in_=self.xT[:, c, ts],
                                                                 func=AF.Square),
                   r=(self.xdep[c][t],), w=(self.sqdep[i],))
                op(PE, lambda e, i=i, c=c, pt=pt: e.matmul(pt[:, :], lhsT=self.ones[:, :], rhs=self.sq[i][:, :],
                                                           start=(c == 0), stop=(c == 7)),
                   r=(self.sqdep[i], self.cdep), w=(pd,))
            op(ACT, lambda e, pt=pt: e.activation(out=self.rstd[:, :], in_=pt[:, :], func=AF.Sqrt,
                                                  bias=self.epsc[:, :], scale=1.0 / D),
               r=(pd, self.cdep), w=(self.rdep,))
            op(DVE, lambda e: e.reciprocal(out=self.rstd[:, :], in_=self.rstd[:, :]), r=(self.rdep,), w=(self.rdep,))
            for c in range(8):
                op(DVE, lambda e, c=c, ts=ts: e.scalar_tensor_tensor(
                    out=self.hT[:, c, ts], in0=self.xT[:, c, ts], scalar=self.gvec[:, c:c + 1], in1=self.rstd[:, :],
                    op0=ALU.mult, op1=ALU.mult),
                   r=(self.xdep[c][t], self.gdep, self.rdep), w=(self.hdep[t],))

    def ffn(self, l, j):
        op = self.op
        inp = self.inp
        self.ar.release(0)
        aT, adep = self.ar.alloc([128, 11, L], BF16, ndeps=44)
        adep = [[adep[f * 4 + t] for t in range(4)] for f in range(11)]
        self.norm_to_hT(inp["ffn_norm"][l, j, :])
        wg = inp["ffn_w_gate"][l, j].rearrange("(c p) f -> p c f", p=128)
        wu = inp["ffn_w_up"][l, j].rearrange("(c p) f -> p c f", p=128)
        wd = inp["ffn_w_down"][l, j].rearrange("(fc p) d -> p fc d", p=128)
        for half in range(2):
            for fi in range(11):
                f = half * 11 + fi
                fs = slice(f * 128, (f + 1) * 128)
                wb, wdep = self.ws.load([(wg[:, :, fs], (8, 128)), (wu[:, :, fs], (8, 128))])
                for t in range(4):
                    ts = slice(t * 512, (t + 1) * 512)
                    pg, pgd = self.pbank()
                    pu, pud = self.pbank()

                    def mm(e, wb=wb, ts=ts, pg=pg, pu=pu):
                        for c in range(8):
                            e.matmul(pg[:, :], lhsT=wb[:, c * 128:(c + 1) * 128], rhs=self.hT[:, c, ts],
                                     start=(c == 0), stop=(c == 7))
                        for c in range(8):
                            ins = e.matmul(pu[:, :], lhsT=wb[:, 1024 + c * 128:1024 + (c + 1) * 128],
                                           rhs=self.hT[:, c, ts], start=(c == 0), stop=(c == 7))
                        return ins
                    op(PE, mm, r=(wdep, self.hdep[t]), w=(pgd, pud))
                    i = self.sgi % 2
                    self.sgi += 1
                    op(ACT, lambda e, i=i, pg=pg: e.activation(out=self.sg[i][:, :], in_=pg[:, :], func=AF.Silu),
                       r=(pgd,), w=(self.sgdep[i],))
                    op(DVE, lambda e, i=i, pu=pu, fi=fi, ts=ts: e.tensor_tensor(
                        out=aT[:, fi, ts], in0=self.sg[i][:, :], in1=pu[:, :], op=ALU.mult),
                       r=(self.sgdep[i], pud), w=(adep[fi][t],))
            for dc in range(8):
                ds = slice(dc * 128, (dc + 1) * 128)
                wb, wdep = self.ws.load([(wd[:, half * 11:(half + 1) * 11, ds], (11, 128))])
                for t in range(4):
                    ts = slice(t * 512, (t + 1) * 512)
                    py, pyd = self.pbank()

                    def mm2(e, wb=wb, ts=ts, py=py):
                        for fi in range(11):
                            ins = e.matmul(py[:, :], lhsT=wb[:, fi * 128:(fi + 1) * 128], rhs=aT[:, fi, ts],
                                           start=(fi == 0), stop=(fi == 10))
                        return ins
                    op(PE, mm2, r=(wdep,) + tuple(adep[fi][t] for fi in range(11)), w=(pyd,))
                    op(DVE, lambda e, dc=dc, ts=ts, py=py: e.scalar_tensor_tensor(
                        out=self.xT[:, dc, ts], in0=py[:, :], scalar=0.5, in1=self.xT[:, dc, ts],
                        op0=ALU.mult, op1=ALU.add),
                       r=(pyd, self.xdep[dc][t]), w=(self.xdep[dc][t],))

    def proj_fm(self, wb, wdep, off, M, t, pt=None, pd=None, prow=0):
        if pt is None:
            pt, pd = self.pbank()
        ts = slice(t * 512, (t + 1) * 512)

        def mm(e):
            for c in range(8):
                ins = e.matmul(pt[prow:prow + M, :], lhsT=wb[:, off + c * M:off + (c + 1) * M],
                               rhs=self.hT[:, c, ts], start=(c == 0), stop=(c == 7))
            return ins
        self.op(PE, mm, r=(wdep, self.hdep[t]), w=(pd,))
        return pt, pd

    def out_accum(self, wb, wdep, nk, rhs_fn, rdeps, t):
        ts = slice(t * 512, (t + 1) * 512)
        for dc in range(8):
            py, pyd = self.pbank(4, 6)

            def mm(e, dc=dc, py=py):
                for kk in range(nk):
                    ins = e.matmul(py[:, :], lhsT=wb[:, kk * 1024 + dc * 128:kk * 1024 + (dc + 1) * 128],
                                   rhs=rhs_fn(kk), start=(kk == 0), stop=(kk == nk - 1))
                return ins
            self.op(PE, mm, r=(wdep,) + tuple(rdeps), w=(pyd,))
            self.op(DVE, lambda e, dc=dc, py=py: e.tensor_tensor(out=self.xT[:, dc, ts], in0=py[:, :],
                                                                 in1=self.xT[:, dc, ts], op=ALU.add),
                    r=(pyd, self.xdep[dc][t]), w=(self.xdep[dc][t],))

    def hybrid(self):
        inp = self.inp
        ar = self.ar
        ar.release(0)
        self.norm_to_hT(inp["mix_norm"][0, :])
        w_in = inp["hy_w_in"][0].rearrange("(c p) f -> p c f", p=128)
        w_out = inp["hy_w_out"][0].rearrange("(kc p) d -> p kc d", p=128)
        tok, tokd = ar.alloc([128, 16, 44], F32)
        m0 = ar.mark()
        import os
        parts = os.environ.get("HY_PARTS", "gates,ssd,mlstm").split(",")
        if "gates" in parts:
            self.hy_gates(w_in, tok, tokd)
        ar.release(m0)
        if "ssd" in parts:
            self.hy_ssd(w_in, w_out, tok, tokd)
        ar.release(m0)
        if "mlstm" in parts:
            self.hy_mlstm(w_in, w_out, tok, tokd)

    def hy_gates(self, w_in, tok, tokd):
        op = self.op
        ar = self.ar
        wb, wdep = self.ws.load([(w_in[:, :, C_DT:C_DT + 16], (8, 16)), (w_in[:, :, C_I:C_I + 4], (8, 4)),
                                 (w_in[:, :, C_F:C_F + 4], (8, 4))])
        e1, e1d = ar.alloc([16, 512], F32)
        ah, ahd = ar.alloc([16, 512], F32)
        acs = [ar.alloc([16, 512], F32) for _ in range(2)]
        e2, e2d = ar.alloc([4, 512], F32)
        sph, sphd = ar.alloc([4, 512], F32)
        cs = [ar.alloc([4, 512], F32) for _ in range(2)]
        iv, ivd = ar.alloc([4, 512], F32)
        imf, imfd = ar.alloc([4, 512], F32)
        for t in range(4):
            ts = slice(t * 512, (t + 1) * 512)
            ac, acd = acs[t % 2]
            acp, acpd = acs[(t + 1) % 2]
            cc, ccd = cs[t % 2]
            ccp, ccpd = cs[(t + 1) % 2]
            pdt, pdd = self.proj_fm(wb, wdep, 0, 16, t)
            op(ACT, lambda e, pdt=pdt: e.activation(out=e1, in_=pdt[0:16, :], func=AF.Exp, bias=self.dtb[:, :], scale=1.0),
               r=(pdd, self.cdep), w=(e1d,))
            op(ACT, lambda e: e.activation(out=e1, in_=e1, func=AF.Ln, bias=1.0, scale=1.0), r=(e1d,), w=(e1d,))
            op(DVE, lambda e: e.tensor_scalar(out=ah, in0=e1, scalar1=self.ahalf[:, :], scalar2=None, op0=ALU.mult),
               r=(e1d, self.cdep), w=(ahd,))
            init = 0.0 if t == 0 else acp[:, 511:512]
            op(DVE, lambda e, ac=ac, init=init: e.tensor_tensor_scan(out=ac, data0=ah, data1=ah, initial=init,
                                                                    op0=ALU.add, op1=ALU.add),
               r=(ahd, acpd), w=(acd,))
            op(SP, lambda e, ac=ac, ts=ts: e.dma_start(out=self.scr[0:16, ts], in_=ac), r=(acd,), w=(self.scrdep,),
               dma="scrw")
            pf, pfd = self.proj_fm(wb, wdep, 160, 4, t)
            op(ACT, lambda e, pf=pf: e.activation(out=e2, in_=pf[0:4, :], func=AF.Exp, bias=self.nfgb[:, :], scale=-1.0),
               r=(pfd, self.cdep), w=(e2d,))
            op(ACT, lambda e: e.activation(out=e2, in_=e2, func=AF.Ln, bias=1.0, scale=1.0), r=(e2d,), w=(e2d,))
            op(DVE, lambda e: e.tensor_scalar(out=sph, in0=e2, scalar1=0.5, scalar2=None, op0=ALU.mult),
               r=(e2d,), w=(sphd,))
            init2 = 0.0 if t == 0 else ccp[:, 511:512]
            op(DVE, lambda e, cc=cc, init2=init2: e.tensor_tensor_scan(out=cc, data0=sph, data1=sph, initial=init2,
                                                                      op0=ALU.add, op1=ALU.add),
               r=(sphd, ccpd), w=(ccd,))
            op(SP, lambda e, cc=cc, ts=ts: e.dma_start(out=self.scr[16:20, ts], in_=cc), r=(ccd,), w=(self.scrdep,),
               dma="scrw")
            pi_, pid = self.proj_fm(wb, wdep, 128, 4, t)
            op(ACT, lambda e, pi_=pi_: e.activation(out=iv, in_=pi_[0:4, :], func=AF.Identity, bias=self.igb[:, :],
                                                    scale=1.0), r=(pid, self.cdep), w=(ivd,))
            op(DVE, lambda e, cc=cc: e.tensor_tensor(out=imf, in0=iv, in1=cc, op=ALU.add), r=(ivd, ccd), w=(imfd,))
            for b in range(4):
                tt = t * 4 + b
                bs = slice(b * 128, (b + 1) * 128)
                pt, ptd = self.pbank()

                def tr(e, pt=pt, bs=bs, ac=ac, cc=cc):
                    e.transpose(out=pt[:, 0:16], in_=ac[:, bs], identity=self.identf[0:16, 0:16])
                    e.transpose(out=pt[:, 16:32], in_=e1[:, bs], identity=self.identf[0:16, 0:16])
                    e.transpose(out=pt[:, 32:36], in_=cc[:, bs], identity=self.identf[0:4, 0:4])
                    e.transpose(out=pt[:, 36:40], in_=iv[:, bs], identity=self.identf[0:4, 0:4])
                    return e.transpose(out=pt[:, 40:44], in_=imf[:, bs], identity=self.identf[0:4, 0:4])
                op(PE, tr, r=(acd, e1d, ccd, ivd, imfd, self.cdep), w=(ptd,))
                op(ACT, lambda e, pt=pt, tt=tt: e.activation(out=tok[:, tt, 0:16], in_=pt[:, 0:16], func=AF.Copy,
                                                             scale=-1.0), r=(ptd,), w=(tokd,))
                op(DVE, lambda e, pt=pt, tt=tt: e.tensor_copy(out=tok[:, tt, 16:44], in_=pt[:, 16:44]),
                   r=(ptd,), w=(tokd,))

    def conv_chunk(self, w_in, col0, cc, out_ap, out_dep):
        op = self.op
        ar = self.ar
        m = ar.mark()
        u, ud = ar.alloc([128, L + 3], F32)
        acc, accd = ar.alloc([128, L], F32)
        wb, wdep = self.ws.load([(w_in[:, :, col0:col0 + 128], (8, 128))])
        op(POOL, lambda e: e.memset(u[:, 0:3], 0.0), w=(ud,))
        for t in range(4):
            pt, pd = self.proj_fm(wb, wdep, 0, 128, t)
            op(ACT, lambda e, pt=pt, t=t: e.activation(out=u[:, 3 + t * 512:3 + (t + 1) * 512], in_=pt[:, :],
                                                       func=AF.Copy), r=(pd,), w=(ud,))
        op(DVE, lambda e: e.tensor_scalar(out=acc, in0=u[:, 3:3 + L], scalar1=self.convw[:, 3, cc:cc + 1],
                                          scalar2=self.convb[:, cc:cc + 1], op0=ALU.mult, op1=ALU.add),
           r=(ud, self.cdep), w=(accd,))
        for kx in (2, 1, 0):
            op(DVE, lambda e, kx=kx: e.scalar_tensor_tensor(out=acc, in0=u[:, kx:kx + L],
                                                           scalar=self.convw[:, kx, cc:cc + 1], in1=acc,
                                                           op0=ALU.mult, op1=ALU.add),
               r=(ud, accd, self.cdep), w=(accd,))
        op(ACT, lambda e: e.activation(out=out_ap, in_=acc, func=AF.Silu), r=(accd,), w=(out_dep,))
        ar.release(m)

    def hy_ssd(self, w_in, w_out, tok, tokd):
        op = self.op
        ar = self.ar
        for g in range(2):
            mg = ar.mark()
            BT, BTd = ar.alloc([128, L], BF16)
            CT, CTd = ar.alloc([128, L], BF16)
            ygf, ygfd = ar.alloc([128, 2, L], BF16)
            self.conv_chunk(w_in, C_B + g * 128, 8 + g, BT, BTd)
            self.conv_chunk(w_in, C_C + g * 128, 10 + g, CT, CTd)
            for half in range(2):
                mu = ar.mark()
                xsT, xsTd = ar.alloc([128, 2, L], BF16, ndeps=2)
                xd, xdd = ar.alloc([128, 2, 16, 128], BF16, ndeps=2)
                ch0 = g * 4 + half * 2
                h0 = ch0 * 2
                for kk in range(2):
                    ch = ch0 + kk
                    self.conv_chunk(w_in, C_XS + ch * 128, ch, xsT[:, kk, :], xsTd[kk])
                    for tt in range(16):
                        pt, ptd = self.pT()
                        op(PE, lambda e, pt=pt, kk=kk, tt=tt: e.transpose(
                            out=pt, in_=xsT[:, kk, tt * 128:(tt + 1) * 128], identity=self.identb[:, :]),
                           r=(xsTd[kk], self.cdep), w=(ptd,))
                        for hh in range(2):
                            head = ch * 2 + hh
                            op(DVE, lambda e, pt=pt, kk=kk, tt=tt, hh=hh, head=head: e.tensor_scalar(
                                out=xd[:, kk, tt, hh * 64:(hh + 1) * 64], in0=pt[:, hh * 64:(hh + 1) * 64],
                                scalar1=tok[:, tt, 16 + head:17 + head], scalar2=None, op0=ALU.mult),
                               r=(ptd, tokd), w=(xdd[kk],))
                acsrow, acsrowd = ar.alloc([128, 4, 512], F32)
                cbm, cbmd = ar.alloc([128, 512], BF16)
                lts = [ar.alloc([128, 512], BF16) for _ in range(2)]
                mts = [ar.alloc([128, 512], BF16) for _ in range(2)]
                dtmp, dtmpd = ar.alloc([128, 128], F32)
                yv, yvd = ar.alloc([128, 512], F32)
                zs, zsd = ar.alloc([128, 512], F32)
                ygc, ygcd = ar.alloc([128, 2, 512], F32)
                yn, ynd = ar.alloc([128, 4, 512], BF16)
                li = 0
                for t in range(4):
                    ts = slice(t * 512, (t + 1) * 512)
                    src = bass.AP(self.scr.tensor, h0 * L + t * 512, [[0, 128], [L, 4], [1, 512]])
                    op(SP, lambda e, src=src: e.dma_start(out=acsrow, in_=src), r=(self.scrdep,), w=(acsrowd,),
                       dma="acsrow")
                    yacc = [(self.psum[0], self.pdep[0]), (self.psum[1], self.pdep[1])]
                    nts = 4 * t + 4
                    for tsb in range(nts):
                        c0 = max(0, tsb * 128 - t * 512)
                        t0 = t * 512 + c0
                        diag = tsb >= 4 * t
                        sb = slice(tsb * 128, (tsb + 1) * 128)
                        pcb, pcbd = self.pbank(2, 4)
                        op(PE, lambda e, pcb=pcb, c0=c0, sb=sb, t0=t0, t=t: e.matmul(
                            pcb[:, c0:512], lhsT=BT[:, sb], rhs=CT[:, t0:(t + 1) * 512], start=True, stop=True),
                           r=(BTd, CTd), w=(pcbd,))
                        if diag:
                            op(DVE, lambda e, pcb=pcb, c0=c0: e.tensor_tensor(
                                out=cbm[:, c0:c0 + 128], in0=pcb[:, c0:c0 + 128], in1=self.trif[:, :], op=ALU.mult),
                               r=(pcbd, self.cdep), w=(cbmd,))
                            if c0 + 128 < 512:
                                op(ACT, lambda e, pcb=pcb, c0=c0: e.activation(
                                    out=cbm[:, c0 + 128:512], in_=pcb[:, c0 + 128:512], func=AF.Copy),
                                   r=(pcbd,), w=(cbmd,))
                        else:
                            op(ACT, lambda e, pcb=pcb: e.activation(out=cbm[:, :], in_=pcb[:, :], func=AF.Copy),
                               r=(pcbd,), w=(cbmd,))
                        for e4 in range(4):
                            kk, hh = e4 // 2, e4 % 2
                            head = h0 + e4
                            ltb, ltd = lts[li % 2]
                            mtb, mtd = mts[li % 2]
                            li += 1
                            nb = tok[:, tsb, head:head + 1]
                            cst = c0
                            if diag:
                                op(DVE, lambda e, e4=e4, c0=c0, nb=nb: e.tensor_scalar(
                                    out=dtmp, in0=acsrow[:, e4, c0:c0 + 128], scalar1=nb, scalar2=0.0,
                                    op0=ALU.add, op1=ALU.min), r=(acsrowd, tokd), w=(dtmpd,))
                                op(ACT, lambda e, ltb=ltb, c0=c0: e.activation(
                                    out=ltb[:, c0:c0 + 128], in_=dtmp, func=AF.Exp), r=(dtmpd,), w=(ltd,))
                                cst = c0 + 128
                            if cst < 512:
                                op(ACT, lambda e, ltb=ltb, e4=e4, cst=cst, nb=nb: e.activation(
                                    out=ltb[:, cst:512], in_=acsrow[:, e4, cst:512], func=AF.Exp, bias=nb, scale=1.0),
                                   r=(acsrowd, tokd), w=(ltd,))
                            op(DVE, lambda e, ltb=ltb, mtb=mtb, c0=c0: e.tensor_tensor(
                                out=mtb[:, c0:512], in0=ltb[:, c0:512], in1=cbm[:, c0:512], op=ALU.mult),
                               r=(ltd, cbmd), w=(mtd,))
                            op(PE, lambda e, mtb=mtb, kk=kk, hh=hh, tsb=tsb, c0=c0, nts=nts, ya=yacc[kk][0]: e.matmul(
                                ya[hh * 64:(hh + 1) * 64, c0:512], lhsT=xd[:, kk, tsb, hh * 64:(hh + 1) * 64],
                                rhs=mtb[:, c0:512], start=(tsb == 0), stop=(tsb == nts - 1)),
                               r=(mtd, xdd[kk]), w=(yacc[kk][1],))
                    for kk in range(2):
                        ch = ch0 + kk
                        wbz, wdz = self.ws.load([(w_in[:, :, C_Z + ch * 128:C_Z + (ch + 1) * 128], (8, 128))])
                        pz, pzd = self.pbank(4, 6)
                        self.proj_fm(wbz, wdz, 0, 128, t, pz, pzd)
                        op(ACT, lambda e, pz=pz: e.activation(out=zs, in_=pz[:, :], func=AF.Silu), r=(pzd,), w=(zsd,))
                        op(DVE, lambda e, kk=kk, ch=ch, ts=ts, ya=yacc[kk][0]: e.scalar_tensor_tensor(
                            out=yv, in0=xsT[:, kk, ts], scalar=self.dskip[:, ch:ch + 1], in1=ya[:, :],
                            op0=ALU.mult, op1=ALU.add), r=(xsTd[kk], yacc[kk][1], self.cdep), w=(yvd,))
                        if half == 0:
                            op(DVE, lambda e, kk=kk, ts=ts: e.tensor_tensor(out=ygf[:, kk, ts], in0=yv, in1=zs,
                                                                           op=ALU.mult), r=(yvd, zsd), w=(ygfd,))
                        else:
                            op(DVE, lambda e, kk=kk: e.tensor_tensor(out=ygc[:, kk, :], in0=yv, in1=zs, op=ALU.mult),
                               r=(yvd, zsd), w=(ygcd,))
                    if half == 1:
                        srcs = [(ygf[:, 0, ts], ygfd), (ygf[:, 1, ts], ygfd), (ygc[:, 0, :], ygcd), (ygc[:, 1, :], ygcd)]
                        pss, pssd = self.pbank(4, 6)
                        for j4, (sap, sd) in enumerate(srcs):
                            i = self.sqi % 2
                            self.sqi += 1
                            op(ACT, lambda e, i=i, sap=sap: e.activation(out=self.sq[i][:, :], in_=sap, func=AF.Square),
                               r=(sd,), w=(self.sqdep[i],))
                            op(PE, lambda e, i=i, j4=j4, pss=pss: e.matmul(pss[:, :], lhsT=self.ones[:, :],
                                                                           rhs=self.sq[i][:, :], start=(j4 == 0),
                                                                           stop=(j4 == 3)),
                               r=(self.sqdep[i], self.cdep), w=(pssd,))
                        op(ACT, lambda e, pss=pss: e.activation(out=self.rstd[:, :], in_=pss[:, :], func=AF.Sqrt,
                                                                bias=self.epsc[:, :], scale=1.0 / 512),
                           r=(pssd, self.cdep), w=(self.rdep,))
                        op(DVE, lambda e: e.reciprocal(out=self.rstd[:, :], in_=self.rstd[:, :]),
                           r=(self.rdep,), w=(self.rdep,))
                        for j4, (sap, sd) in enumerate(srcs):
                            ch = g * 4 + j4
                            op(DVE, lambda e, j4=j4, sap=sap, ch=ch: e.scalar_tensor_tensor(
                                out=yn[:, j4, :], in0=sap, scalar=self.ssdn[:, ch:ch + 1], in1=self.rstd[:, :],
                                op0=ALU.mult, op1=ALU.mult), r=(sd, self.rdep, self.cdep), w=(ynd,))
                        for hf in range(2):
                            r0 = g * 4 + hf * 2
                            wbo, wdo = self.ws.load([(w_out[:, r0:r0 + 2, :], (2, 1024))])
                            self.out_accum(wbo, wdo, 2, lambda kk, hf=hf: yn[:, hf * 2 + kk, :], (ynd,), t)
                ar.release(mu)
            ar.release(mg)

    def hy_mlstm(self, w_in, w_out, tok, tokd):
        for hm in range(4):
            self.hy_mlstm_head(w_in, w_out, tok, tokd, hm)

    def hy_mlstm_head(self, w_in, w_out, tok, tokd, hm):
        op = self.op
        ar = self.ar
        if True:
            m = ar.mark()
            qT, qTd = ar.alloc([128, L], BF16)
            kT, kTd = ar.alloc([128, L], BF16)
            vtok, vtokd = ar.alloc([128, 16, 256], BF16)
            csrow, csrowd = ar.alloc([128, 512], F32)
            qkm, qkmd = ar.alloc([128, 512], BF16)
            lts = [ar.alloc([128, 512], BF16) for _ in range(2)]
            sts = [ar.alloc([128, 512], BF16) for _ in range(2)]
            dtmp, dtmpd = ar.alloc([128, 128], F32)
            hv, hvd = ar.alloc([128, 2, 512], F32)
            rden, rdend = ar.alloc([128, 512], F32)
            og, ogd = ar.alloc([128, 512], F32)
            tmpn, tmpnd = ar.alloc([128, 512], F32)
            hn, hnd = ar.alloc([128, 2, 512], BF16)
            wb, wd_ = self.ws.load([(w_in[:, :, C_Q + hm * 128:C_Q + (hm + 1) * 128], (8, 128)),
                                    (w_in[:, :, C_K + hm * 128:C_K + (hm + 1) * 128], (8, 128))])
            for t in range(4):
                ts = slice(t * 512, (t + 1) * 512)
                pq, pqd = self.proj_fm(wb, wd_, 0, 128, t)
                op(ACT, lambda e, pq=pq, ts=ts: e.activation(out=qT[:, ts], in_=pq[:, :], func=AF.Copy,
                                                             scale=128 ** -0.5), r=(pqd,), w=(qTd,))
                pk, pkd = self.proj_fm(wb, wd_, 1024, 128, t)
                op(DVE, lambda e, pk=pk, ts=ts: e.tensor_copy(out=kT[:, ts], in_=pk[:, :]), r=(pkd,), w=(kTd,))
            wbv, wdv = self.ws.load([(w_in[:, :, C_V + hm * 256:C_V + (hm + 1) * 256], (8, 256))])
            for tt in range(16):
                pv, pvd = self.pbank()

                def mmv(e, pv=pv, tt=tt):
                    for c in range(8):
                        ins = e.matmul(pv[:, 0:256], lhsT=self.hT[:, c, tt * 128:(tt + 1) * 128],
                                       rhs=wbv[:, c * 256:(c + 1) * 256], start=(c == 0), stop=(c == 7))
                    return ins
                op(PE, mmv, r=(wdv, self.hdep[tt // 4]), w=(pvd,))
                if tt % 2 == 0:
                    op(ACT, lambda e, pv=pv, tt=tt: e.activation(out=vtok[:, tt, :], in_=pv[:, 0:256], func=AF.Copy),
                       r=(pvd,), w=(vtokd,))
                else:
                    op(DVE, lambda e, pv=pv, tt=tt: e.tensor_copy(out=vtok[:, tt, :], in_=pv[:, 0:256]),
                       r=(pvd,), w=(vtokd,))
            li = 0
            for t in range(4):
                ts = slice(t * 512, (t + 1) * 512)
                op(SP, lambda e, ts=ts: e.dma_start(out=csrow, in_=self.scr[16 + hm:17 + hm, ts].partition_broadcast(128)),
                   r=(self.scrdep,), w=(csrowd,), dma="csrow")
                acc = [(self.psum[i], self.pdep[i]) for i in range(3)]
                nts = 4 * t + 4
                for tsb in range(nts):
                    c0 = max(0, tsb * 128 - t * 512)
                    t0 = t * 512 + c0
                    diag = tsb >= 4 * t
                    sb = slice(tsb * 128, (tsb + 1) * 128)
                    pqk, pqkd = self.pbank(3, 5)
                    op(PE, lambda e, pqk=pqk, c0=c0, sb=sb, t0=t0, t=t: e.matmul(
                        pqk[:, c0:512], lhsT=kT[:, sb], rhs=qT[:, t0:(t + 1) * 512], start=True, stop=True),
                       r=(kTd, qTd), w=(pqkd,))
                    if diag:
                        op(DVE, lambda e, pqk=pqk, c0=c0: e.tensor_tensor(
                            out=qkm[:, c0:c0 + 128], in0=pqk[:, c0:c0 + 128], in1=self.trif[:, :], op=ALU.mult),
                           r=(pqkd, self.cdep), w=(qkmd,))
                        if c0 + 128 < 512:
                            op(ACT, lambda e, pqk=pqk, c0=c0: e.activation(
                                out=qkm[:, c0 + 128:512], in_=pqk[:, c0 + 128:512], func=AF.Copy),
                               r=(pqkd,), w=(qkmd,))
                    else:
                        op(ACT, lambda e, pqk=pqk: e.activation(out=qkm[:, :], in_=pqk[:, :], func=AF.Copy),
                           r=(pqkd,), w=(qkmd,))
                    ltb, ltd = lts[li % 2]
                    stb, std = sts[li % 2]
                    li += 1
                    cst = c0
                    if diag:
                        op(DVE, lambda e, c0=c0, tsb=tsb: e.tensor_scalar(
                            out=dtmp, in0=csrow[:, c0:c0 + 128], scalar1=tok[:, tsb, 32 + hm:33 + hm], scalar2=0.0,
                            op0=ALU.subtract, op1=ALU.max), r=(csrowd, tokd), w=(dtmpd,))
                        op(ACT, lambda e, ltb=ltb, c0=c0, tsb=tsb: e.activation(
                            out=ltb[:, c0:c0 + 128], in_=dtmp, func=AF.Exp, bias=tok[:, tsb, 36 + hm:37 + hm],
                            scale=-1.0), r=(dtmpd, tokd), w=(ltd,))
                        cst = c0 + 128
                    if cst < 512:
                        op(ACT, lambda e, ltb=ltb, cst=cst, tsb=tsb: e.activation(
                            out=ltb[:, cst:512], in_=csrow[:, cst:512], func=AF.Exp,
                            bias=tok[:, tsb, 40 + hm:41 + hm], scale=-1.0), r=(csrowd, tokd), w=(ltd,))
                    op(DVE, lambda e, ltb=ltb, stb=stb, c0=c0: e.tensor_tensor(
                        out=stb[:, c0:512], in0=ltb[:, c0:512], in1=qkm[:, c0:512], op=ALU.mult),
                       r=(ltd, qkmd), w=(std,))

                    def mmn(e, stb=stb, tsb=tsb, c0=c0, nts=nts, acc=acc):
                        st_, sp_ = (tsb == 0), (tsb == nts - 1)
                        e.matmul(acc[0][0][:, c0:512], lhsT=vtok[:, tsb, 0:128], rhs=stb[:, c0:512], start=st_, stop=sp_)
                        e.matmul(acc[1][0][:, c0:512], lhsT=vtok[:, tsb, 128:256], rhs=stb[:, c0:512], start=st_, stop=sp_)
                        return e.matmul(acc[2][0][:, c0:512], lhsT=self.ones[:, :], rhs=stb[:, c0:512], start=st_,
                                        stop=sp_)
                    op(PE, mmn, r=(std, vtokd, self.cdep), w=(acc[0][1], acc[1][1], acc[2][1]))
                op(ACT, lambda e, acc=acc: e.activation(out=rden, in_=acc[2][0][:, :], func=AF.Abs),
                   r=(acc[2][1],), w=(rdend,))
                op(DVE, lambda e: e.tensor_scalar(out=rden, in0=rden, scalar1=1.0, scalar2=None, op0=ALU.max),
                   r=(rdend,), w=(rdend,))
                op(DVE, lambda e: e.reciprocal(out=rden, in_=rden), r=(rdend,), w=(rdend,))
                pss, pssd = self.pbank(5, 6)
                for j in range(2):
                    op(DVE, lambda e, j=j, acc=acc: e.tensor_tensor(out=hv[:, j, :], in0=acc[j][0][:, :], in1=rden,
                                                                    op=ALU.mult), r=(acc[j][1], rdend), w=(hvd,))
                    i = self.sqi % 2
                    self.sqi += 1
                    op(ACT, lambda e, i=i, j=j: e.activation(out=self.sq[i][:, :], in_=hv[:, j, :], func=AF.Square),
                       r=(hvd,), w=(self.sqdep[i],))
                    op(PE, lambda e, i=i, j=j, pss=pss: e.matmul(pss[:, :], lhsT=self.ones[:, :], rhs=self.sq[i][:, :],
                                                                 start=(j == 0), stop=(j == 1)),
                       r=(self.sqdep[i], self.cdep), w=(pssd,))
                op(ACT, lambda e, pss=pss: e.activation(out=self.rstd[:, :], in_=pss[:, :], func=AF.Sqrt,
                                                        bias=self.epsc[:, :], scale=1.0 / 256),
                   r=(pssd, self.cdep), w=(self.rdep,))
                op(DVE, lambda e: e.reciprocal(out=self.rstd[:, :], in_=self.rstd[:, :]), r=(self.rdep,), w=(self.rdep,))
                wbo, wdo = self.ws.load([(w_in[:, :, C_O + hm * 256:C_O + hm * 256 + 128], (8, 128)),
                                         (w_in[:, :, C_O + hm * 256 + 128:C_O + hm * 256 + 256], (8, 128))])
                for j in range(2):
                    po, pod = self.pbank(5, 6)
                    self.proj_fm(wbo, wdo, j * 1024, 128, t, po, pod)
                    op(ACT, lambda e, po=po: e.activation(out=og, in_=po[:, :], func=AF.Sigmoid), r=(pod,), w=(ogd,))
                    op(DVE, lambda e, j=j: e.scalar_tensor_tensor(
                        out=tmpn, in0=hv[:, j, :], scalar=self.mln[:, hm * 2 + j:hm * 2 + j + 1], in1=self.rstd[:, :],
                        op0=ALU.mult, op1=ALU.mult), r=(hvd, self.rdep, self.cdep), w=(tmpnd,))
                    op(DVE, lambda e, j=j: e.tensor_tensor(out=hn[:, j, :], in0=tmpn, in1=og, op=ALU.mult),
                       r=(tmpnd, ogd), w=(hnd,))
                r0 = 8 + hm * 2
                wbw, wdw = self.ws.load([(w_out[:, r0:r0 + 2, :], (2, 1024))])
                self.out_accum(wbw, wdw, 2, lambda kk: hn[:, kk, :], (hnd,), t)
            ar.release(m)

    def sin_reduced(self, ang, out, tmps, shift):
        import math
        op = self.op
        (nf, nfd), (r, rd), (angd,) = tmps
        twopi = 2.0 * math.pi
        MAGIC = 12582912.0
        op(DVE, lambda e: e.tensor_scalar(out=nf, in0=ang, scalar1=shift, scalar2=1.0 / twopi, op0=ALU.add,
                                          op1=ALU.mult), r=(angd,), w=(nfd,))
        op(DVE, lambda e: e.tensor_scalar(out=nf, in0=nf, scalar1=MAGIC, scalar2=None, op0=ALU.add), r=(nfd,), w=(nfd,))
        op(DVE, lambda e: e.tensor_scalar(out=nf, in0=nf, scalar1=-MAGIC, scalar2=None, op0=ALU.add), r=(nfd,), w=(nfd,))
        op(DVE, lambda e: e.scalar_tensor_tensor(out=r, in0=nf, scalar=-twopi, in1=ang, op0=ALU.mult, op1=ALU.add),
           r=(nfd, angd), w=(rd,))
        lo, hi = -math.pi + shift * 0.0, math.pi
        if shift != 0.0:
            op(DVE, lambda e: e.tensor_scalar(out=r, in0=r, scalar1=shift, scalar2=None, op0=ALU.add), r=(rd,), w=(rd,))
        op(DVE, lambda e: e.tensor_scalar(out=r, in0=r, scalar1=-3.1415925, scalar2=3.1415925, op0=ALU.max, op1=ALU.min),
           r=(rd,), w=(rd,))
        op(ACT, lambda e: e.activation(out=out[0], in_=r, func=AF.Sin), r=(rd,), w=(out[1],))

    def sa_tables(self, q, t, cosT, cosd, sinT, sind):
        op = self.op
        ar = self.ar
        m = ar.mark()
        posi, posid = ar.alloc([128, 512], I32)
        ang, angd = ar.alloc([128, 512], F32)
        nf = ar.alloc([128, 512], F32)
        r = ar.alloc([128, 512], F32)
        ts = slice(t * 512, (t + 1) * 512)
        op(SP, lambda e: e.dma_start(out=posi, in_=self.inp["positions"][q:q + 1, ts].partition_broadcast(128)),
           w=(posid,), dma="pos")
        op(DVE, lambda e: e.tensor_copy(out=ang, in_=posi), r=(posid,), w=(angd,))
        op(DVE, lambda e: e.tensor_scalar(out=ang, in0=ang, scalar1=self.invf, scalar2=None, op0=ALU.mult),
           r=(angd, self.cdep), w=(angd,))
        import math
        self.sin_reduced(ang, (sinT, sind), (nf, r, (angd,)), 0.0)
        self.sin_reduced(ang, (cosT, cosd), (nf, r, (angd,)), math.pi / 2)
        ar.release(m)

    def sa_rope_chunk(self, wb, wdep, off, t, gain, cosT, cosd, sinT, sind, out_ap, out_dep, tm, dup=False):
        op = self.op
        (xb, xbd), (t1, t1d), (t2, t2d) = tm
        pq, pqd = self.pbank()
        if dup:
            self.proj_fm(wb, wdep, off, 64, t, pq, pqd, prow=0)
            self.proj_fm(wb, wdep, off, 64, t, pq, pqd, prow=64)
        else:
            self.proj_fm(wb, wdep, off, 128, t, pq, pqd)
        if gain is not None:
            i = self.sqi % 2
            self.sqi += 1
            op(ACT, lambda e, i=i: e.activation(out=self.sq[i][:, :], in_=pq[:, :], func=AF.Square),
               r=(pqd,), w=(self.sqdep[i],))
            pss, pssd = self.pbank()
            op(PE, lambda e, i=i: e.matmul(pss[:, :], lhsT=self.bd[:, :], rhs=self.sq[i][:, :], start=True, stop=True),
               r=(self.sqdep[i], self.cdep), w=(pssd,))
            op(ACT, lambda e: e.activation(out=self.rstd[:, :], in_=pss[:, :], func=AF.Sqrt, bias=self.epsc[:, :],
                                           scale=1.0 / 64), r=(pssd, self.cdep), w=(self.rdep,))
            op(DVE, lambda e: e.reciprocal(out=self.rstd[:, :], in_=self.rstd[:, :]), r=(self.rdep,), w=(self.rdep,))
            op(DVE, lambda e: e.scalar_tensor_tensor(out=xb, in0=pq[:, :], scalar=gain[:, :], in1=self.rstd[:, :],
                                                     op0=ALU.mult, op1=ALU.mult),
               r=(pqd, self.rdep, self.cdep), w=(xbd,))
        else:
            op(ACT, lambda e: e.activation(out=xb, in_=pq[:, :], func=AF.Copy), r=(pqd,), w=(xbd,))
        pr, prd = self.pbank()
        op(PE, lambda e: e.matmul(pr[:, :], lhsT=self.rotm[:, :], rhs=xb, start=True, stop=True),
           r=(xbd, self.cdep), w=(prd,))
        op(DVE, lambda e: e.tensor_tensor(out=t1, in0=xb, in1=cosT, op=ALU.mult), r=(xbd, cosd), w=(t1d,))
        op(DVE, lambda e: e.tensor_tensor(out=t2, in0=pr[:, :], in1=sinT, op=ALU.mult), r=(prd, sind), w=(t2d,))
        op(DVE, lambda e: e.tensor_tensor(out=out_ap, in0=t1, in1=t2, op=ALU.add), r=(t1d, t2d), w=(out_dep,))

    def sattn(self, q):
        op = self.op
        inp = self.inp
        ar = self.ar
        NIT = self.NIT
        import os
        sa_parts = os.environ.get("SA_PARTS", "prep,idx,att").split(",")
        ar.release(0)
        self.norm_to_hT(inp["mix_norm"][1, :])
        w_in = inp["sa_w_in"][0].rearrange("(c p) f -> p c f", p=128)
        w_out = inp["sa_w_out"][0].rearrange("(c p) d -> p c d", p=128)
        kk2 = [ar.alloc([128, L], BF16) for _ in range(2)]
        ki2 = [ar.alloc([128, L], BF16) for _ in range(2)]
        vext, vextd = ar.alloc([128, 16, 66], BF16)
        witok, witokd = ar.alloc([128, 16, 8], F32)
        qT, qTd = ar.alloc([128, 8, 512], BF16)
        qiT, qiTd = ar.alloc([128, 4, 512], BF16)
        oT, oTd = ar.alloc([128, 8, 512], BF16)
        m1 = ar.mark()
        cosT, cosd = ar.alloc([128, 512], F32)
        sinT, sind = ar.alloc([128, 512], F32)
        kd, kdd = ar.alloc([128, 512], BF16)
        tm = [ar.alloc([128, 512], BF16), ar.alloc([128, 512], F32), ar.alloc([128, 512], F32)]
        wbk, wdk = self.ws.load([(w_in[:, :, S_K:S_K + 64], (8, 64)), (w_in[:, :, S_KI:S_KI + 64], (8, 64))])
        for t in range(4):
            ts = slice(t * 512, (t + 1) * 512)
            self.sa_tables(q, t, cosT, cosd, sinT, sind)
            for woff, gain, dst in ((0, self.kng, kk2), (512, None, ki2)):
                self.sa_rope_chunk(wbk, wdk, woff, t, gain, cosT, cosd, sinT, sind, kd, kdd, tm, dup=True)
                for hh in range(2):
                    op(DVE, lambda e, hh=hh, dst=dst, ts=ts: e.tensor_scalar(
                        out=dst[hh][0][:, ts], in0=kd, scalar1=self.mcol[hh][:, :], scalar2=None, op0=ALU.mult),
                       r=(kdd, self.cdep), w=(dst[hh][1],))
        wbv, wdv = self.ws.load([(w_in[:, :, S_V:S_V + 64], (8, 64)), (w_in[:, :, S_WI:S_WI + 8], (8, 8))])
        op(POOL, lambda e: e.memset(vext[:, :, 64:66], 1.0), w=(vextd,))
        for tt in range(16):
            pv, pvd = self.pbank()

            def mmv(e, pv=pv, tt=tt):
                for c in range(8):
                    e.matmul(pv[:, 0:64], lhsT=self.hT[:, c, tt * 128:(tt + 1) * 128],
                             rhs=wbv[:, c * 64:(c + 1) * 64], start=(c == 0), stop=(c == 7))
                for c in range(8):
                    ins = e.matmul(pv[:, 64:72], lhsT=self.hT[:, c, tt * 128:(tt + 1) * 128],
                                   rhs=wbv[:, 512 + c * 8:512 + (c + 1) * 8], start=(c == 0), stop=(c == 7))
                return ins
            op(PE, mmv, r=(wdv, self.hdep[tt // 4]), w=(pvd,))
            op(ACT, lambda e, pv=pv, tt=tt: e.activation(out=vext[:, tt, 0:64], in_=pv[:, 0:64], func=AF.Copy),
               r=(pvd,), w=(vextd,))
            op(DVE, lambda e, pv=pv, tt=tt: e.tensor_copy(out=witok[:, tt, :], in_=pv[:, 64:72]), r=(pvd,), w=(witokd,))
        ar.release(m1)
        for t in range(4):
            ts = slice(t * 512, (t + 1) * 512)
            m2 = ar.mark()
            cosT, cosd = ar.alloc([128, 512], F32)
            sinT, sind = ar.alloc([128, 512], F32)
            tm = [ar.alloc([128, 512], BF16), ar.alloc([128, 512], F32), ar.alloc([128, 512], F32)]
            self.sa_tables(q, t, cosT, cosd, sinT, sind)
            for c in range(8):
                wbq, wdq = self.ws.load([(w_in[:, :, S_Q + c * 128:S_Q + (c + 1) * 128], (8, 128))])
                self.sa_rope_chunk(wbq, wdq, 0, t, self.qng, cosT, cosd, sinT, sind, qT[:, c, :], qTd, tm)
            for c in range(4):
                wbq, wdq = self.ws.load([(w_in[:, :, S_QI + c * 128:S_QI + (c + 1) * 128], (8, 128))])
                self.sa_rope_chunk(wbq, wdq, 0, t, None, cosT, cosd, sinT, sind, qiT[:, c, :], qiTd, tm)
            ar.release(m2)
            sc, scd = ar.alloc([128, L], F32)
            sel, seld = ar.alloc([128, L], BF16)
            junk, junkd = sel, seld
            scn, scnd = sel, seld
            selT, selTd = ar.alloc([128, 16, 128], BF16)
            PTa = [ar.alloc([128, 512], BF16) for _ in range(1)]
            PTb = [ar.alloc([128, 512], BF16) for _ in range(1)]
            otok, otokd = ar.alloc([128, 1024], BF16)
            rr = [ar.alloc([128, 512], F32) for _ in range(1)]
            sm, smd = ar.alloc([128, 64], F32)
            hwc, hwcd = ar.alloc([128, NIT + 1], F32)
            rcp, rcpd = ar.alloc([128, 16], F32)
            mx, mn, rng, mid, cnt, sgn, thr = [sm[:, i:i + 1] for i in range(7)]
            ri = 0
            pi2 = 0
            for ql in range(4):
                qb = t * 4 + ql
                qs = slice(ql * 128, (ql + 1) * 128)
                W = (qb + 1) * 128
                nkt = (W + 511) // 512
                if "idx" in sa_parts:
                    for j in range(nkt):
                        wc = min(512, W - j * 512)
                        for ih in range(8):
                            ci, hh = ih // 2, ih % 2
                            pl, pld = self.pbank(3, 6)
                            op(PE, lambda e, pl=pl, wc=wc, ci=ci, hh=hh, j=j, qs=qs: e.matmul(
                                pl[:, 0:wc], lhsT=qiT[:, ci, qs],
                                rhs=ki2[hh][0][:, j * 512:j * 512 + wc], start=True, stop=True),
                               r=(qiTd, ki2[hh][1]), w=(pld,))
                            rb, rbd = rr[0]
                            ri += 1
                            op(ACT, lambda e, pl=pl, rb=rb, wc=wc: e.activation(out=rb[:, 0:wc], in_=pl[:, 0:wc],
                                                                                func=AF.Relu), r=(pld,), w=(rbd,))
                            wcol = witok[:, qb, ih:ih + 1]
                            scs = sc[:, j * 512:j * 512 + wc]
                            if ih == 0:
                                op(DVE, lambda e, rb=rb, wc=wc, wcol=wcol, scs=scs: e.tensor_scalar(
                                    out=scs, in0=rb[:, 0:wc], scalar1=wcol, scalar2=None, op0=ALU.mult),
                                   r=(rbd, witokd), w=(scd,))
                            else:
                                op(DVE, lambda e, rb=rb, wc=wc, wcol=wcol, scs=scs: e.scalar_tensor_tensor(
                                    out=scs, in0=rb[:, 0:wc], scalar=wcol, in1=scs, op0=ALU.mult, op1=ALU.add),
                                   r=(rbd, witokd, scd), w=(scd,))
                    if qb >= 2:
                        op(DVE, lambda e, W=W: e.tensor_tensor_scan(out=scn[:, 0:W], data0=sc[:, 0:W], data1=sc[:, 0:W],
                                                                    initial=-1e30, op0=ALU.max, op1=ALU.max),
                           r=(scd,), w=(scnd,))
                        op(DVE, lambda e, W=W: e.tensor_copy(out=mx, in_=scn[:, W - 1:W]), r=(scnd,), w=(smd,))
                        op(DVE, lambda e, W=W: e.tensor_tensor_scan(out=scn[:, 0:W], data0=sc[:, 0:W], data1=sc[:, 0:W],
                                                                    initial=1e30, op0=ALU.min, op1=ALU.min),
                           r=(scd, smd), w=(scnd,))
                        op(DVE, lambda e, W=W: e.tensor_copy(out=mn, in_=scn[:, W - 1:W]), r=(scnd,), w=(smd,))
                    op(POOL, lambda e, W=W: e.affine_select(out=sc[:, W - 128:W], in_=sc[:, W - 128:W], pattern=[[-1, 128]],
                                                            compare_op=ALU.is_ge, fill=self.negreg(e), base=0, channel_multiplier=1),
                       r=(scd, smd), w=(scd,))
                    if qb >= 2:
                        op(DVE, lambda e: e.tensor_tensor(out=rng, in0=mx, in1=mn, op=ALU.subtract), r=(smd,), w=(smd,))
                        op(DVE, lambda e: e.scalar_tensor_tensor(out=mn, in0=rng, scalar=-0.02, in1=mn, op0=ALU.mult,
                                                                 op1=ALU.add), r=(smd,), w=(smd,))
                        op(DVE, lambda e: e.tensor_scalar(out=rng, in0=rng, scalar1=1.04, scalar2=None, op0=ALU.mult),
                           r=(smd,), w=(smd,))
                        op(DVE, lambda e: e.tensor_scalar(out=hwc, in0=self.pwr[:, :], scalar1=rng, scalar2=None,
                                                          op0=ALU.mult), r=(smd, self.cdep), w=(hwcd,))
                        op(DVE, lambda e: e.tensor_tensor(out=mid, in0=mn, in1=hwc[:, 0:1], op=ALU.add),
                           r=(smd, hwcd), w=(smd,))
                        for it in range(NIT):
                            op(DVE, lambda e, W=W: e.tensor_scalar(out=junk[:, 0:W], in0=sc[:, 0:W], scalar1=mid, scalar2=0.0,
                                                                   op0=ALU.is_ge, op1=ALU.add, accum_out=cnt),
                               r=(scd, smd), w=(junkd, smd))
                            op(DVE, lambda e: e.tensor_scalar(out=sgn, in0=cnt, scalar1=255.5, scalar2=0.5, op0=ALU.is_ge,
                                                              op1=ALU.subtract), r=(smd,), w=(smd,))
                            op(DVE, lambda e, it=it: e.scalar_tensor_tensor(out=mid, in0=sgn, scalar=hwc[:, it:it + 1],
                                                                            in1=mid, op0=ALU.mult, op1=ALU.add),
                               r=(smd, hwcd), w=(smd,))
                        op(DVE, lambda e: e.tensor_tensor(out=thr, in0=mid, in1=hwc[:, NIT:NIT + 1], op=ALU.subtract),
                           r=(smd, hwcd), w=(smd,))
                    else:
                        op(DVE, lambda e: e.memset(thr, -1e29), w=(smd,))
                    op(DVE, lambda e, W=W: e.tensor_scalar(out=sel[:, 0:W], in0=sc[:, 0:W], scalar1=thr, scalar2=None,
                                                           op0=ALU.is_ge), r=(scd, smd), w=(seld,))
                else:
                    op(DVE, lambda e, W=W: e.memset(sel[:, 0:W], 1.0), w=(seld,))
                if "att" not in sa_parts:
                    continue
                for kt in range(qb + 1):
                    pt, ptd = self.pT()
                    op(PE, lambda e, pt=pt, kt=kt: e.transpose(out=pt, in_=sel[:, kt * 128:(kt + 1) * 128],
                                                               identity=self.identb[:, :]),
                       r=(seld, self.cdep), w=(ptd,))
                    op(ACT, lambda e, pt=pt, kt=kt: e.activation(out=selT[:, kt, :], in_=pt, func=AF.Copy),
                       r=(ptd,), w=(selTd,))
                pacc = [(self.psum[i], self.pdep[i]) for i in range(3)]
                for kt in range(qb + 1):
                    ks = slice(kt * 128, (kt + 1) * 128)
                    for hg in range(4):
                        ps_, psd = self.pbank(3, 6)

                        def mms(e, ps_=ps_, hg=hg, ks=ks, qs=qs):
                            for hl in range(4):
                                h = 2 * ((hg // 2) * 4 + hl) + (hg % 2)
                                c, hh = h // 2, h % 2
                                ins = e.matmul(ps_[:, hl * 128:(hl + 1) * 128], lhsT=kk2[hh][0][:, ks],
                                               rhs=qT[:, c, qs], start=True, stop=True)
                            return ins
                        op(PE, mms, r=(kk2[0][1], kk2[1][1], qTd), w=(psd,))
                        pa, pad = PTa[0]
                        pb, pbd = PTb[0]
                        pi2 += 1
                        op(ACT, lambda e, ps_=ps_, pa=pa: e.activation(out=pa, in_=ps_[:, :], func=AF.Exp, scale=0.125),
                           r=(psd,), w=(pad,))
                        op(DVE, lambda e, pa=pa, pb=pb, kt=kt: e.tensor_tensor(
                            out=pb.rearrange("p (a b) -> p a b", a=4), in0=pa.rearrange("p (a b) -> p a b", a=4),
                            in1=selT[:, kt:kt + 1, :].broadcast_to([128, 4, 128]), op=ALU.mult),
                           r=(pad, selTd), w=(pbd,))

                        def mmp(e, pb=pb, hg=hg, kt=kt, qb=qb, pacc=pacc):
                            for hl in range(4):
                                h = 2 * ((hg // 2) * 4 + hl) + (hg % 2)
                                bank, slot = h // 7, h % 7
                                ins = e.matmul(pacc[bank][0][:, slot * 65:slot * 65 + 65],
                                               lhsT=pb[:, hl * 128:(hl + 1) * 128], rhs=vext[:, kt, 0:65],
                                               start=(kt == 0 and slot == 0), stop=(kt == qb), skip_group_check=True)
                            return ins
                        op(PE, mmp, r=(pbd, vextd), w=tuple(d for _, d in pacc))
                for b, nh in enumerate((7, 7, 2)):
                    pv3 = pacc[b][0][:, 0:nh * 65].rearrange("p (a b) -> p a b", a=nh)
                    h0 = b * 7
                    op(DVE, lambda e, pv3=pv3, h0=h0, nh=nh: e.reciprocal(
                        out=rcp[:, h0:h0 + nh].rearrange("p (a b) -> p a b", b=1), in_=pv3[:, :, 64:65]),
                       r=(pacc[b][1],), w=(rcpd,))
                    op(DVE, lambda e, pv3=pv3, h0=h0, nh=nh: e.tensor_tensor(
                        out=otok[:, h0 * 64:(h0 + nh) * 64].rearrange("p (a b) -> p a b", a=nh), in0=pv3[:, :, 0:64],
                        in1=rcp[:, h0:h0 + nh].rearrange("p (a b) -> p a b", b=1).broadcast_to([128, nh, 64]),
                        op=ALU.mult), r=(pacc[b][1], rcpd), w=(otokd,))
                for c in range(8):
                    pt, ptd = self.pT()
                    op(PE, lambda e, pt=pt, c=c: e.transpose(out=pt, in_=otok[:, c * 128:(c + 1) * 128],
                                                             identity=self.identb[:, :]),
                       r=(otokd, self.cdep), w=(ptd,))
                    op(ACT, lambda e, pt=pt, c=c, qs=qs: e.activation(out=oT[:, c, qs], in_=pt, func=AF.Copy),
                       r=(ptd,), w=(oTd,))
            if "att" not in sa_parts:
                ar.release(m2)
                continue
            for dc in range(8):
                wbo, wdo = self.ws.load([(w_out[:, :, dc * 128:(dc + 1) * 128], (8, 128))])
                py, pyd = self.pbank(3, 6)

                def mmo(e, wbo=wbo, py=py):
                    for c in range(8):
                        ins = e.matmul(py[:, :], lhsT=wbo[:, c * 128:(c + 1) * 128], rhs=oT[:, c, :],
                                       start=(c == 0), stop=(c == 7))
                    return ins
                op(PE, mmo, r=(wdo, oTd), w=(pyd,))
                op(DVE, lambda e, dc=dc, py=py, ts=ts: e.tensor_tensor(out=self.xT[:, dc, ts], in0=py[:, :],
                                                                       in1=self.xT[:, dc, ts], op=ALU.add),
                   r=(pyd, self.xdep[dc][t]), w=(self.xdep[dc][t],))
            ar.release(m2)


WEIGHT_NAMES = [
    ("ffn_norm", (2, 2, D), F32), ("ffn_w_gate", (2, 2, D, DFF), F32), ("ffn_w_up", (2, 2, D, DFF), F32),
    ("ffn_w_down", (2, 2, DFF, D), F32), ("mix_norm", (2, D), F32),
    ("hy_w_in", (1, D, 5656), F32), ("hy_conv_w", (1, 4, 1536), F32), ("hy_conv_b", (1, 1536), F32),
    ("hy_dt_bias", (1, 16), F32), ("hy_a_log", (1, 16), F32), ("hy_d_skip", (1, 16), F32),
    ("hy_ssd_norm", (1, 1024), F32), ("hy_igate_bias", (1, 4), F32), ("hy_fgate_bias", (1, 4), F32),
    ("hy_mlstm_norm", (1, 1024), F32), ("hy_w_out", (1, 2048, D), F32),
    ("sa_w_in", (1, D, 1736), F32), ("sa_q_norm", (1, 64), F32), ("sa_k_norm", (1, 64), F32),
    ("sa_w_out", (1, D, D), F32), ("positions", (4, L), I32),
]
STAGE_INPUTS = {
    "ffn": ["ffn_norm", "ffn_w_gate", "ffn_w_up", "ffn_w_down"],
    "hy": ["mix_norm", "hy_w_in", "hy_conv_w", "hy_conv_b", "hy_dt_bias", "hy_a_log", "hy_d_skip", "hy_ssd_norm",
           "hy_igate_bias", "hy_fgate_bias", "hy_mlstm_norm", "hy_w_out"],
    "sa": ["mix_norm", "sa_w_in", "sa_q_norm", "sa_k_norm", "sa_w_out", "positions"],
}


def build(nseq, stages):
    need = set()
    for st in stages:
        need.update(STAGE_INPUTS[st[0]])
    names = [("xT", (nseq, D, L), F32)]
    for nm, shape, dt in WEIGHT_NAMES:
        if nm in need:
            if nm == "positions":
                shape = (nseq, L)
            names.append((nm, shape, dt))
    p = Prog(nseq, stages, names)
    return p.build(), [n[0] for n in names]


FULL_STAGES = [("ffn", 0, 0), ("hy",), ("ffn", 0, 1), ("ffn", 1, 0), ("sa",), ("ffn", 1, 1)]
_PROG_CACHE = {}


def kernel(**inputs):
    nseq = 4
    key = "full"
    if key not in _PROG_CACHE:
        _PROG_CACHE[key] = build(nseq, FULL_STAGES)
    nc, names = _PROG_CACHE[key]
    x = np.asarray(inputs["x"], dtype=np.float32)
    pos = np.asarray(inputs["positions"], dtype=np.int32)
    in_maps = []
    for c in range(NCORES):
        m = {}
        for n in names:
            if n == "xT":
                m[n] = np.ascontiguousarray(x[c * nseq:(c + 1) * nseq].transpose(0, 2, 1))
            elif n == "positions":
                m[n] = np.ascontiguousarray(pos[c * nseq:(c + 1) * nseq])
            else:
                m[n] = np.ascontiguousarray(np.asarray(inputs[n], dtype=np.float32))
        in_maps.append(m)
    res = run_bass_kernel_spmd(nc, in_maps, core_ids=list(range(NCORES)))
    out = np.empty((NCORES * nseq, L, D), dtype=np.float32)
    for c in range(NCORES):
        out[c * nseq:(c + 1) * nseq] = res.results[c]["yT"].transpose(0, 2, 1)
    return out
```
